# Optimizing a Trainium2 kernel written in Bass

```python
import jax
import jax.numpy as jnp
from jax import lax
import numpy as np

D_MODEL = 1024
BATCH = 4
SEQ = 8192
DEPTH = 2

MLA_HEADS = 8
MLA_Q_LORA = 384
MLA_KV_LORA = 256
MLA_NOPE_DIM = 64
MLA_ROPE_DIM = 32
MLA_V_DIM = 64
MLA_WIDTH = MLA_HEADS * MLA_V_DIM
ATTN_Q_BLOCK = 128

LRU_WIDTH = 512
LRU_BLOCKS = 8
LRU_BLOCK_DIM = LRU_WIDTH // LRU_BLOCKS
LRU_CONV_WIDTH = 4
LRU_C = 8.0

RET_HEADS = 8
RET_KEY_DIM = 64
RET_VAL_DIM = 64
RET_WIDTH = RET_HEADS * RET_VAL_DIM
RET_CHUNK = 128

N_BRANCHES = 3
BRANCH_WIDTH = 512
ROPE_BASE = 10000.0
LN_EPS = 1e-5
RMS_EPS = 1e-6
DEEPNORM_ALPHA = (2 * DEPTH) ** 0.25
DEEPNORM_BETA = (8 * DEPTH) ** -0.25

IN_SIZES = (MLA_Q_LORA, MLA_KV_LORA, MLA_ROPE_DIM, MLA_WIDTH,
            LRU_WIDTH, LRU_WIDTH,
            RET_HEADS * RET_KEY_DIM, RET_HEADS * RET_KEY_DIM, RET_WIDTH, RET_WIDTH,
            N_BRANCHES * D_MODEL)
IN_COLS = sum(IN_SIZES)
IN_SPLITS = tuple(int(s) for s in np.cumsum(IN_SIZES)[:-1])

kernel_name = 'hybrid_mla_rglru_retention_deepnorm'


def _layer_norm(x, g, b):
    xf = x.astype(jnp.float32)
    mu = jnp.mean(xf, -1, keepdims=True)
    var = jnp.mean(jnp.square(xf - mu), -1, keepdims=True)
    return ((xf - mu) * lax.rsqrt(var + LN_EPS) * g + b).astype(x.dtype)


def _rms_norm(x, g):
    xf = x.astype(jnp.float32)
    return (xf * lax.rsqrt(jnp.mean(jnp.square(xf), -1, keepdims=True) + RMS_EPS) * g).astype(x.dtype)


def _rope(x, pos):
    half = x.shape[-1] // 2
    inv_freq = ROPE_BASE ** (-jnp.arange(half, dtype=jnp.float32) / half)
    ang = pos.astype(jnp.float32)[:, :, None, None] * inv_freq
    cos, sin = jnp.cos(ang), jnp.sin(ang)
    x1 = x[..., :half].astype(jnp.float32)
    x2 = x[..., half:].astype(jnp.float32)
    return jnp.concatenate([x1 * cos - x2 * sin, x2 * cos + x1 * sin], -1).astype(x.dtype)


def _causal_block_attention(q, k, v, scale):
    B, S, H, dq = q.shape
    nb = S // ATTN_Q_BLOCK
    q_blocks = q.reshape(B, nb, ATTN_Q_BLOCK, H, dq).transpose(1, 0, 2, 3, 4)
    starts = jnp.arange(nb, dtype=jnp.int32) * ATTN_Q_BLOCK
    key_idx = jnp.arange(S, dtype=jnp.int32)

    def one_block(args):
        qb, s0 = args
        s = jnp.einsum('bqhd,bkhd->bhqk', qb, k).astype(jnp.float32) * scale
        q_idx = s0 + jnp.arange(ATTN_Q_BLOCK, dtype=jnp.int32)
        mask = key_idx[None, :] <= q_idx[:, None]
        p = jax.nn.softmax(jnp.where(mask, s, -jnp.inf), axis=-1).astype(v.dtype)
        return jnp.einsum('bhqk,bkhd->bqhd', p, v)

    out = lax.map(one_block, (q_blocks, starts))
    return out.transpose(1, 0, 2, 3, 4).reshape(B, S, H, v.shape[-1])


def _mla(q_lat, kv_lat, k_pe, pos, q_norm, w_uq, kv_norm, w_ukv):
    B, S, _ = q_lat.shape
    q = jnp.einsum('bsr,re->bse', _rms_norm(q_lat, q_norm), w_uq)
    q = q.reshape(B, S, MLA_HEADS, MLA_NOPE_DIM + MLA_ROPE_DIM)
    q = jnp.concatenate([q[..., :MLA_NOPE_DIM], _rope(q[..., MLA_NOPE_DIM:], pos)], -1)
    kv = jnp.einsum('bsr,re->bse', _rms_norm(kv_lat, kv_norm), w_ukv)
    kv = kv.reshape(B, S, MLA_HEADS, MLA_NOPE_DIM + MLA_V_DIM)
    k_nope, v = kv[..., :MLA_NOPE_DIM], kv[..., MLA_NOPE_DIM:]
    k_rot = jnp.broadcast_to(_rope(k_pe[:, :, None, :], pos), (B, S, MLA_HEADS, MLA_ROPE_DIM))
    k = jnp.concatenate([k_nope, k_rot], -1)
    o = _causal_block_attention(q, k, v, (MLA_NOPE_DIM + MLA_ROPE_DIM) ** -0.5)
    return o.reshape(B, S, MLA_WIDTH)


def _rglru(u, conv_w, conv_b, w_r, b_r, w_i, b_i, lam):
    B, S, W = u.shape
    xc = lax.conv_general_dilated(u, conv_w[:, None, :], window_strides=(1,),
                                  padding=[(LRU_CONV_WIDTH - 1, 0)],
                                  dimension_numbers=('NWC', 'WIO', 'NWC'),
                                  feature_group_count=W) + conv_b
    xh = xc.reshape(B, S, LRU_BLOCKS, LRU_BLOCK_DIM)
    r = jax.nn.sigmoid(jnp.einsum('bshi,hij->bshj', xh, w_r) + b_r.reshape(LRU_BLOCKS, LRU_BLOCK_DIM)).reshape(B, S, W)
    gi = jax.nn.sigmoid(jnp.einsum('bshi,hij->bshj', xh, w_i) + b_i.reshape(LRU_BLOCKS, LRU_BLOCK_DIM)).reshape(B, S, W)
    log_a = -LRU_C * r.astype(jnp.float32) * jax.nn.softplus(-lam.astype(jnp.float32))
    a = jnp.exp(log_a)
    b = jnp.sqrt(-jnp.expm1(2.0 * log_a)) * (gi * xc).astype(jnp.float32)

    def combine(left, right):
        a1, b1 = left
        a2, b2 = right
        return a1 * a2, a2 * b1 + b2

    _, h = lax.associative_scan(combine, (a, b), axis=1)
    return h.astype(u.dtype)


def _retention(q, k, v, pos, gn_g):
    B, S, _ = q.shape
    H, C = RET_HEADS, RET_CHUNK
    N = S // C
    q = _rope(q.reshape(B, S, H, RET_KEY_DIM), pos)
    k = _rope(k.reshape(B, S, H, RET_KEY_DIM), pos) * (RET_KEY_DIM ** -0.5)
    v = v.reshape(B, S, H, RET_VAL_DIM)
    log_gamma = jnp.log1p(-jnp.exp2(-5.0 - jnp.arange(H, dtype=jnp.float32)))
    idx = jnp.arange(C, dtype=jnp.float32)
    diff = idx[:, None] - idx[None, :]
    decay = jnp.where(diff[None] >= 0, jnp.exp(diff[None] * log_gamma[:, None, None]), 0.0)
    qc = q.reshape(B, N, C, H, RET_KEY_DIM)
    kc = k.reshape(B, N, C, H, RET_KEY_DIM)
    vc = v.reshape(B, N, C, H, RET_VAL_DIM)
    scores = jnp.einsum('bnihd,bnjhd->bnhij', qc, kc) * decay.astype(q.dtype)
    inner = jnp.einsum('bnhij,bnjhe->bnihe', scores, vc)
    k_w = jnp.exp((C - 1.0 - idx)[:, None] * log_gamma[None, :])
    chunk_kv = jnp.einsum('bnjhd,bnjhe->bnhde', kc * k_w[:, :, None].astype(k.dtype), vc).astype(jnp.float32)
    chunk_decay = jnp.exp(C * log_gamma)[None, :, None, None]

    def step(state, kv_n):
        return chunk_decay * state + kv_n, state

    _, prev = lax.scan(step, jnp.zeros((B, H, RET_KEY_DIM, RET_VAL_DIM), jnp.float32),
                       jnp.moveaxis(chunk_kv, 1, 0))
    prev = jnp.moveaxis(prev, 0, 1).astype(q.dtype)
    q_w = jnp.exp((idx + 1.0)[:, None] * log_gamma[None, :])
    cross = jnp.einsum('bnihd,bnhde->bnihe', qc * q_w[:, :, None].astype(q.dtype), prev)
    o = (inner + cross).reshape(B, S, H, RET_VAL_DIM).astype(jnp.float32)
    mu = jnp.mean(o, -1, keepdims=True)
    var = jnp.mean(jnp.square(o - mu), -1, keepdims=True)
    o = (o - mu) * lax.rsqrt(var + LN_EPS) * gn_g.reshape(H, RET_VAL_DIM)
    return o.reshape(B, S, RET_WIDTH).astype(v.dtype)


def setup_inputs(seed: int = 0) -> dict:
    key = jax.random.key(seed)
    ks = jax.random.split(key, 20)
    f32 = jnp.float32

    def nrm(k, shape, scale):
        return jax.random.normal(k, shape, f32) * scale

    x = nrm(ks[0], (BATCH, SEQ, D_MODEL), 1.0)
    positions = (jnp.arange(SEQ, dtype=jnp.int32)[None, :]
                 + jax.random.randint(ks[1], (BATCH, 1), 0, 1024, dtype=jnp.int32))
    w_in = nrm(ks[2], (DEPTH, D_MODEL, IN_COLS), D_MODEL ** -0.5)
    b_merge = nrm(ks[3], (DEPTH, N_BRANCHES * D_MODEL), 0.02)
    mla_q_norm = 1.0 + nrm(ks[4], (DEPTH, MLA_Q_LORA), 0.02)
    mla_w_uq = nrm(ks[5], (DEPTH, MLA_Q_LORA, MLA_HEADS * (MLA_NOPE_DIM + MLA_ROPE_DIM)), MLA_Q_LORA ** -0.5)
    mla_kv_norm = 1.0 + nrm(ks[6], (DEPTH, MLA_KV_LORA), 0.02)
    mla_w_ukv = nrm(ks[7], (DEPTH, MLA_KV_LORA, MLA_HEADS * (MLA_NOPE_DIM + MLA_V_DIM)), MLA_KV_LORA ** -0.5)
    lru_conv_w = nrm(ks[8], (DEPTH, LRU_CONV_WIDTH, LRU_WIDTH), LRU_CONV_WIDTH ** -0.5)
    lru_conv_b = nrm(ks[9], (DEPTH, LRU_WIDTH), 0.02)
    lru_w_r = nrm(ks[10], (DEPTH, LRU_BLOCKS, LRU_BLOCK_DIM, LRU_BLOCK_DIM), LRU_BLOCK_DIM ** -0.5)
    lru_b_r = nrm(ks[11], (DEPTH, LRU_WIDTH), 0.02)
    lru_w_i = nrm(ks[12], (DEPTH, LRU_BLOCKS, LRU_BLOCK_DIM, LRU_BLOCK_DIM), LRU_BLOCK_DIM ** -0.5)
    lru_b_i = nrm(ks[13], (DEPTH, LRU_WIDTH), 0.02)
    a_c = jax.random.uniform(ks[14], (DEPTH, LRU_WIDTH), f32, minval=0.9, maxval=0.999)
    a0 = a_c ** (1.0 / LRU_C)
    lru_lambda = jnp.log(a0) - jnp.log1p(-a0)
    ret_gn_g = 1.0 + nrm(ks[15], (DEPTH, RET_WIDTH), 0.02)
    w_branch = nrm(ks[16], (DEPTH, N_BRANCHES, BRANCH_WIDTH, D_MODEL), BRANCH_WIDTH ** -0.5 * DEEPNORM_BETA)
    w_out = nrm(ks[17], (DEPTH, D_MODEL, D_MODEL), D_MODEL ** -0.5 * DEEPNORM_BETA)
    ln_g = 1.0 + nrm(ks[18], (DEPTH, D_MODEL), 0.02)
    ln_b = nrm(ks[19], (DEPTH, D_MODEL), 0.02)
    return {'x': x, 'positions': positions, 'w_in': w_in, 'b_merge': b_merge,
            'mla_q_norm': mla_q_norm, 'mla_w_uq': mla_w_uq, 'mla_kv_norm': mla_kv_norm, 'mla_w_ukv': mla_w_ukv,
            'lru_conv_w': lru_conv_w, 'lru_conv_b': lru_conv_b, 'lru_w_r': lru_w_r, 'lru_b_r': lru_b_r,
            'lru_w_i': lru_w_i, 'lru_b_i': lru_b_i, 'lru_lambda': lru_lambda, 'ret_gn_g': ret_gn_g,
            'w_branch': w_branch, 'w_out': w_out, 'ln_g': ln_g, 'ln_b': ln_b}


def reference(x, positions, w_in, b_merge, mla_q_norm, mla_w_uq, mla_kv_norm, mla_w_ukv,
              lru_conv_w, lru_conv_b, lru_w_r, lru_b_r, lru_w_i, lru_b_i, lru_lambda, ret_gn_g,
              w_branch, w_out, ln_g, ln_b):
    B, S, _ = x.shape
    for l in range(DEPTH):
        z = jnp.einsum('bsd,de->bse', x, w_in[l])
        (q_lat, kv_lat, k_pe, g_mla, lru_u, g_lru,
         r_q, r_k, r_v, g_ret, merge_logits) = jnp.split(z, IN_SPLITS, axis=-1)
        y_mla = _mla(q_lat, kv_lat, k_pe, positions, mla_q_norm[l], mla_w_uq[l],
                     mla_kv_norm[l], mla_w_ukv[l]) * jax.nn.silu(g_mla)
        y_lru = _rglru(lru_u, lru_conv_w[l], lru_conv_b[l], lru_w_r[l], lru_b_r[l],
                       lru_w_i[l], lru_b_i[l], lru_lambda[l]) * jax.nn.silu(g_lru)
        y_ret = _retention(r_q, r_k, r_v, positions, ret_gn_g[l]) * jax.nn.silu(g_ret)
        gates = jax.nn.sigmoid(merge_logits + b_merge[l]).reshape(B, S, N_BRANCHES, D_MODEL)
        mixed = (gates[:, :, 0] * jnp.einsum('bse,ed->bsd', y_mla, w_branch[l, 0])
                 + gates[:, :, 1] * jnp.einsum('bse,ed->bsd', y_lru, w_branch[l, 1])
                 + gates[:, :, 2] * jnp.einsum('bse,ed->bsd', y_ret, w_branch[l, 2]))
        out = jnp.einsum('bsd,de->bse', mixed, w_out[l])
        x = _layer_norm(DEEPNORM_ALPHA * x + out, ln_g[l], ln_b[l])
    return x
```

```python
import contextlib
import math
import numpy as np
import concourse.bass as bass
import concourse.mybir as mybir
from concourse.bass_utils import run_bass_kernel_spmd

F32 = mybir.dt.float32
BF16 = mybir.dt.bfloat16
I32 = mybir.dt.int32
AF = mybir.ActivationFunctionType
ALU = mybir.AluOpType
AX = mybir.AxisListType

D = 1024
NH = 8
C_QL, C_KVL, C_KPE, C_GM, C_U, C_GL, C_RQ, C_RK, C_RV, C_GR, C_MG = (
    0, 384, 640, 672, 1184, 1696, 2208, 2720, 3232, 3744, 4256)
IN_COLS = 7328
LN_EPS = 1e-5
RMS_EPS = 1e-6
ALPHA = (2 * 2) ** 0.25
LOGG = [math.log1p(-2.0 ** (-5.0 - h)) for h in range(NH)]
TWO_PI = 2.0 * math.pi
CW1 = 6.28125
CW2 = TWO_PI - CW1


class Sched:
    NDS = 8

    def __init__(self, nc, es):
        self.nc = nc
        self.es = es
        self.gen = 0
        self.eng = {'pe': nc.tensor, 'dve': nc.vector, 'act': nc.scalar, 'pool': nc.gpsimd, 'sp': nc.sync}
        self.sem = {}
        self.cnt = {}
        for e in ['pe', 'dve', 'act', 'pool']:
            self.sem[e] = es.enter_context(nc.semaphore("prog_" + e))
            self.cnt[e] = 0
        self.dsem = {}
        self.dcnt = {}
        self.dnext = {}
        for q in ['sp', 'pool', 'act']:
            self.dsem[q] = [es.enter_context(nc.semaphore("dma_%s_%d" % (q, k))) for k in range(self.NDS)]
            self.dcnt[q] = [0] * self.NDS
            self.dnext[q] = 0
        self.seen = {e: {} for e in self.eng}
        self.bufs = {}
        self.same_engine_sync = True
        self.ninst = 0

    def _need(self, e, tok):
        if tok is None:
            return
        sem, val, own = tok
        if own == e and (e == 'pe' or not self.same_engine_sync):
            return
        k = id(sem)
        if self.seen[e].get(k, 0) >= val:
            return
        self.eng[e].wait_ge(sem, val)
        self.seen[e][k] = val

    def _deps(self, e, r, w):
        for k in r:
            b = self.bufs.setdefault(k, {'w': None, 'r': []})
            self._need(e, b['w'])
        for k in w:
            b = self.bufs.setdefault(k, {'w': None, 'r': []})
            self._need(e, b['w'])
            for t in b['r']:
                self._need(e, t)

    def _record(self, tok, r, w):
        for k in r:
            b = self.bufs[k]
            b['r'] = [t for t in b['r'] if t[0] is not tok[0]] + [tok]
        for k in w:
            b = self.bufs[k]
            b['w'] = tok
            b['r'] = []

    def op(self, e, fn, r=(), w=()):
        self._deps(e, r, w)
        ins = fn(self.eng[e])
        self.cnt[e] += 1
        ins.then_inc(self.sem[e], 1)
        self._record((self.sem[e], self.cnt[e], e), r, w)
        self.ninst += 1
        return ins

    def dma(self, q, out, in_, r=(), w=(), **kw):
        self._deps(q, r, w)
        k = self.dnext[q]
        self.dnext[q] = (k + 1) % self.NDS
        sem = self.dsem[q][k]
        if self.dcnt[q][k] > 0:
            self._need(q, (sem, 16 * self.dcnt[q][k], 'dma'))
        ins = self.eng[q].dma_start(out=out, in_=in_, **kw)
        self.dcnt[q][k] += 1
        ins.then_inc(sem, 16)
        self._record((sem, 16 * self.dcnt[q][k], 'dma'), r, w)
        self.ninst += 1
        return ins

    def _all_tokens(self):
        toks = []
        for e in self.cnt:
            if self.cnt[e] > 0:
                toks.append((self.sem[e], self.cnt[e], '*'))
        for q in self.dsem:
            for k in range(self.NDS):
                if self.dcnt[q][k] > 0:
                    toks.append((self.dsem[q][k], 16 * self.dcnt[q][k], '*'))
        return toks

    def barrier(self):
        toks = self._all_tokens()
        for e in self.eng:
            for t in toks:
                self._need(e, t)
        self.bufs = {}
        self.gen += 1
        for e in list(self.sem):
            self.sem[e] = self.es.enter_context(self.nc.semaphore("prog_%s_g%d" % (e, self.gen)))
            self.cnt[e] = 0

    def finish(self, e='sp'):
        for t in self._all_tokens():
            self._need(e, t)


import os
PH = os.environ.get('KPH', 'A1,A2,B,C').split(',')
KSUB = int(os.environ.get('KSUB', '99'))


def build(SEQ, DEPTH):
    NT = SEQ // 128
    NST = SEQ // 512
    nc = bass.Bass("TRN2", target_bir_lowering=False)

    def din(name, shape, dt=F32):
        return nc.dram_tensor(name, shape, dt, kind="ExternalInput").ap()

    x_in = din("x", [SEQ, D])
    pos_in = din("pos", [SEQ, 1], I32)
    w_in = din("w_in", [DEPTH, D, IN_COLS])
    b_merge = din("b_merge", [DEPTH, 128, 24])
    q_norm = din("q_norm", [DEPTH, 128, 3])
    w_uq = din("w_uq", [DEPTH, 384, 768])
    kv_norm = din("kv_norm", [DEPTH, 128, 2])
    w_ukv = din("w_ukv", [DEPTH, 256, 1024])
    conv_w = din("conv_w", [DEPTH, 128, 4, 4])
    lru_vec = din("lru_vec", [DEPTH, 128, 4, 4])
    bd_r = din("bd_r", [DEPTH, 128, 4, 128])
    bd_i = din("bd_i", [DEPTH, 128, 4, 128])
    gn_g = din("gn_g", [DEPTH, 128, 4])
    w_branch = din("w_branch", [DEPTH, 3, 512, D])
    w_out = din("w_out", [DEPTH, D, D])
    ln_g = din("ln_g", [DEPTH, D])
    ln_b = din("ln_b", [DEPTH, D])
    y_out = nc.dram_tensor("y", [SEQ, D], F32, kind="ExternalOutput").ap()

    def dscr(name, shape, dt=BF16):
        return nc.dram_tensor(name, shape, dt).ap()

    XT = dscr("XT", [D, SEQ])
    QT = dscr("QT", [NH, 96, SEQ])
    KT = dscr("KT", [NH, 96, SEQ])
    VV = dscr("VV", [NH, 128, NT, 65])
    GMT = dscr("GMT", [512, SEQ])
    OO = dscr("OO", [SEQ, 512])
    YTL = dscr("YTL", [512, SEQ])
    YTR = dscr("YTR", [512, SEQ])
    X1 = dscr("X1", [SEQ, D], F32)

    with contextlib.ExitStack() as es0:
        S = Sched(nc, es0)
        op = S.op
        ps = es0.enter_context(nc.psum_tensor("ps", [128, 8, 512], F32))

        def psb(b):
            return ps[:, b, :].bitcast(BF16)

        uniq = [0]

        def sb(es, name, shape, dt):
            uniq[0] += 1
            return es.enter_context(nc.sbuf_tensor("%s_%d" % (name, uniq[0]), shape, dt))

        iot = sb(es0, "iot", [128, 128], I32)
        idf = sb(es0, "idf", [128, 128], F32)
        idb = sb(es0, "idb", [128, 128], BF16)
        cmask = sb(es0, "cmask", [128, 128], BF16)
        ones_b = sb(es0, "ones_b", [128, 8], BF16)
        invf = sb(es0, "invf", [128, 48], F32)
        op('pool', lambda e: e.iota(iot[:], pattern=[[1, 128]], base=0, channel_multiplier=-1), w=['iot'])
        op('dve', lambda e: e.tensor_scalar(out=idf[:], in0=iot[:], scalar1=0.0, scalar2=None, op0=ALU.is_equal), r=['iot'], w=['idf'])
        op('dve', lambda e: e.tensor_copy(out=idb[:], in_=idf[:]), r=['idf'], w=['idb'])
        op('dve', lambda e: e.tensor_scalar(out=cmask[:], in0=iot[:], scalar1=0.0, scalar2=None, op0=ALU.is_ge), r=['iot'], w=['cmask'])
        op('pool', lambda e: e.memset(ones_b[:], 1.0), w=['ones_b'])
        for half, off in ((16, 0), (32, 16)):
            fr = np.power(np.float32(10000.0), -np.arange(half, dtype=np.float32) / np.float32(half)).astype(np.float32)
            for j in range(half):
                op('pool', lambda e: e.memset(invf[:, off + j:off + j + 1], float(fr[j])), w=['invf'])
        DT = sb(es0, "DT", [128, 8, 128], BF16)
        qwT = sb(es0, "qwT", [128, 4, 128], F32)
        kw = sb(es0, "kw", [128, 8], F32)
        gCt = sb(es0, "gCt", [128, 4], F32)
        dif = sb(es0, "dif", [128, 128], F32)
        ip1 = sb(es0, "ip1", [128, 128], F32)
        c127 = sb(es0, "c127", [128, 1], F32)
        tmpi = sb(es0, "tmpi", [128, 128], I32)
        op('dve', lambda e: e.tensor_copy(out=dif[:], in_=iot[:]), r=['iot'], w=['dif'])
        op('pool', lambda e: e.iota(tmpi[:], pattern=[[1, 128]], base=1, channel_multiplier=0), w=['tmpi'])
        op('dve', lambda e: e.tensor_copy(out=ip1[:], in_=tmpi[:]), r=['tmpi'], w=['ip1'])
        tmpi2 = sb(es0, "tmpi2", [128, 1], I32)
        op('pool', lambda e: e.iota(tmpi2[:], pattern=[[1, 1]], base=127, channel_multiplier=-1), w=['tmpi2'])
        op('dve', lambda e: e.tensor_copy(out=c127[:], in_=tmpi2[:]), r=['tmpi2'], w=['c127'])
        dtf = sb(es0, "dtf", [128, 128], F32)
        for h in range(NH):
            op('act', lambda e: e.activation(out=dtf[:], in_=dif[:], func=AF.Exp, scale=LOGG[h]), r=['dif'], w=['dtf'])
            op('dve', lambda e: e.tensor_tensor(out=DT[:, h, :], in0=dtf[:], in1=cmask[:], op=ALU.mult), r=['dtf', 'cmask'], w=['DT'])
            p_, hh = h // 2, h % 2
            op('act', lambda e: e.activation(out=qwT[hh * 64:(hh + 1) * 64, p_, :], in_=ip1[hh * 64:(hh + 1) * 64, :], func=AF.Exp, scale=LOGG[h]), r=['ip1'], w=['qwT'])
            op('act', lambda e: e.activation(out=kw[:, h:h + 1], in_=c127[:], func=AF.Exp, scale=LOGG[h]), r=['c127'], w=['kw'])
            op('pool', lambda e: e.memset(gCt[hh * 64:(hh + 1) * 64, p_:p_ + 1], math.exp(128.0 * LOGG[h])), w=['gCt'])

        posi = sb(es0, "posi", [128, 2], I32)
        posf = sb(es0, "posf", [128, 2], F32)
        ang = sb(es0, "ang", [128, 48], F32)
        rr = sb(es0, "rr", [128, 48], F32)
        nn_i = sb(es0, "nn_i", [128, 48], I32)
        nn_f = sb(es0, "nn_f", [128, 48], F32)
        tab = sb(es0, "tab", [128, 2, 160], F32)

        def rope_tables(t, par):
            kp, ka = 'posi%d' % par, 'tab%d' % par
            S.dma('sp', posi[:, par:par + 1], pos_in[t * 128:(t + 1) * 128, :], w=[kp])
            op('dve', lambda e: e.tensor_copy(out=posf[:, par:par + 1], in_=posi[:, par:par + 1]), r=[kp], w=['posf'])
            op('dve', lambda e: e.tensor_scalar(out=ang[:], in0=invf[:], scalar1=posf[:, par:par + 1], scalar2=None, op0=ALU.mult), r=['posf', 'invf'], w=['ang'])
            op('dve', lambda e: e.tensor_scalar(out=rr[:], in0=ang[:], scalar1=1.0 / TWO_PI, scalar2=None, op0=ALU.mult), r=['ang'], w=['rr'])
            op('dve', lambda e: e.tensor_copy(out=nn_i[:], in_=rr[:]), r=['rr'], w=['nn_i'])
            op('dve', lambda e: e.tensor_copy(out=nn_f[:], in_=nn_i[:]), r=['nn_i'], w=['nn_f'])
            op('dve', lambda e: e.scalar_tensor_tensor(out=rr[:], in0=nn_f[:], scalar=-CW1, in1=ang[:], op0=ALU.mult, op1=ALU.add), r=['nn_f', 'ang'], w=['rr'])
            op('dve', lambda e: e.scalar_tensor_tensor(out=rr[:], in0=nn_f[:], scalar=-CW2, in1=rr[:], op0=ALU.mult, op1=ALU.add), r=['nn_f', 'rr'], w=['rr'])
            op('dve', lambda e: e.tensor_scalar(out=rr[:], in0=rr[:], scalar1=math.pi, scalar2=-math.pi, op0=ALU.min, op1=ALU.max), r=['rr'], w=['rr'])
            op('act', lambda e: e.activation(out=tab[:, par, 16:32], in_=rr[:, 0:16], func=AF.Sin), r=['rr'], w=[ka])
            op('act', lambda e: e.activation(out=tab[:, par, 64:96], in_=rr[:, 16:48], func=AF.Sin), r=['rr'], w=[ka])
            op('dve', lambda e: e.scalar_tensor_tensor(out=ang[:], in0=rr[:], scalar=-1.0, in1=rr[:], op0=ALU.mult, op1=ALU.max), r=['rr'], w=['ang'])
            op('dve', lambda e: e.tensor_scalar(out=ang[:], in0=ang[:], scalar1=-1.0, scalar2=math.pi / 2, op0=ALU.mult, op1=ALU.add), r=['ang'], w=['ang'])
            op('act', lambda e: e.activation(out=tab[:, par, 0:16], in_=ang[:, 0:16], func=AF.Sin), r=['ang'], w=[ka])
            op('act', lambda e: e.activation(out=tab[:, par, 32:64], in_=ang[:, 16:48], func=AF.Sin), r=['ang'], w=[ka])
            op('dve', lambda e: e.tensor_scalar(out=tab[:, par, 96:160], in0=tab[:, par, 32:96], scalar1=0.125, scalar2=None, op0=ALU.mult), r=[ka], w=[ka])

        def rope(e_mul, src, srckeys, dst, dstkeys, H, half, cosv, sinv, tabkey, tmp, tmpkey):
            x1 = src[:, :, 0:half]
            x2 = src[:, :, half:2 * half]
            cb = cosv.unsqueeze(1).broadcast_to([128, H, half])
            sbb = sinv.unsqueeze(1).broadcast_to([128, H, half])
            n = H * half
            t = [tmp[:, i, 0:n].rearrange("p (h d) -> p h d", h=H) for i in range(4)]
            op('dve', lambda e: e.tensor_tensor(out=t[0], in0=x1, in1=cb, op=ALU.mult), r=srckeys + [tabkey], w=[tmpkey + '0'])
            op('dve', lambda e: e.tensor_tensor(out=t[1], in0=x2, in1=sbb, op=ALU.mult), r=srckeys + [tabkey], w=[tmpkey + '1'])
            op('dve', lambda e: e.tensor_tensor(out=t[2], in0=x2, in1=cb, op=ALU.mult), r=srckeys + [tabkey], w=[tmpkey + '2'])
            op('dve', lambda e: e.tensor_tensor(out=t[3], in0=x1, in1=sbb, op=ALU.mult), r=srckeys + [tabkey], w=[tmpkey + '3'])
            op(e_mul, lambda e: e.tensor_tensor(out=dst[:, :, 0:half], in0=t[0], in1=t[1], op=ALU.subtract), r=[tmpkey + '0', tmpkey + '1'], w=dstkeys)
            op(e_mul, lambda e: e.tensor_tensor(out=dst[:, :, half:2 * half], in0=t[2], in1=t[3], op=ALU.add), r=[tmpkey + '2', tmpkey + '3'], w=dstkeys)

        def castload(dst, src, ncols, w):
            c0 = 0
            while c0 < ncols:
                n = min(1024, ncols - c0)
                S.dma('pool', dst[:, c0:c0 + n], src[:, c0:c0 + n], w=w)
                c0 += n

        for l in range(DEPTH):
            xsrc = x_in if l == 0 else X1
            ydst = X1 if l < DEPTH - 1 else y_out

            with contextlib.ExitStack() as es:
              if 'A1' in PH:
                wA = sb(es, "wA", [128, 8, 1184], BF16)
                wuq = sb(es, "wuq", [128, 3, 768], BF16)
                wukv = sb(es, "wukv", [128, 2, 1024], BF16)
                stg = sb(es, "stg", [128, 1024], F32)
                qn = sb(es, "qn", [128, 3], F32)
                kvn = sb(es, "kvn", [128, 2], F32)
                xin = sb(es, "xin", [128, 2, 1024], F32)
                xb = sb(es, "xb", [128, 2, 1024], BF16)
                xT = sb(es, "xT", [128, 8, 512], BF16)
                latT = sb(es, "latT", [128, 5, 512], BF16)
                sq = sb(es, "sq", [128, 5, 512], BF16)
                sgm = sb(es, "sgm", [128, 4, 512], BF16)
                Qst = sb(es, "Qst", [128, 8, 96], BF16)
                Kst = sb(es, "Kst", [128, 8, 96], BF16)
                Vst = sb(es, "Vst", [128, 4, 8, 65], BF16)
                QTst = sb(es, "QTst", [128, 8, 512], BF16)
                KTst = sb(es, "KTst", [128, 8, 512], BF16)
                krot = sb(es, "krot", [128, 1, 32], BF16)
                rtmp = sb(es, "rtmp", [128, 4, 256], F32)
                sm = sb(es, "sm", [128, 8], F32)
                ctab = sb(es, "ctab", [128, 32], F32)

                for c in range(8):
                    castload(wA[:, c, :], w_in[l, c * 128:(c + 1) * 128, 0:1184], 1184, ['wA'])
                S.dma('sp', qn[:], q_norm[l], w=['qn'])
                S.dma('sp', kvn[:], kv_norm[l], w=['kvn'])
                for c in range(3):
                    S.dma('sp', stg[:, 0:768], w_uq[l, c * 128:(c + 1) * 128, :], w=['stg'])
                    op('dve', lambda e: e.tensor_scalar(out=wuq[:, c, :], in0=stg[:, 0:768], scalar1=qn[:, c:c + 1], scalar2=None, op0=ALU.mult), r=['stg', 'qn'], w=['wuq'])
                for c in range(2):
                    S.dma('sp', stg[:, :], w_ukv[l, c * 128:(c + 1) * 128, :], w=['stg'])
                    op('dve', lambda e: e.tensor_scalar(out=wukv[:, c, :], in0=stg[:, :], scalar1=kvn[:, c:c + 1], scalar2=None, op0=ALU.mult), r=['stg', 'kvn'], w=['wukv'])
                op('pool', lambda e: e.memset(Vst[:], 1.0), w=['Vst'])

                def load_x(t):
                    S.dma('sp', xin[:, t % 2, :], xsrc[t * 128:(t + 1) * 128, :], w=['xin%d' % (t % 2)])
                load_x(0)
                for st in range(NST if KSUB >= 1 else 0):
                    for tt in range(4):
                        t = st * 4 + tt
                        par = t % 2
                        if t + 1 < NT:
                            load_x(t + 1)
                        op('act', lambda e: e.activation(out=xb[:, par, :], in_=xin[:, par, :], func=AF.Copy), r=['xin%d' % par], w=['xb%d' % par])
                        pb = psb(2)
                        for c in range(8):
                            op('pe', lambda e: e.transpose(out=pb[:, c * 128:(c + 1) * 128], in_=xb[:, par, c * 128:(c + 1) * 128], identity=idb[:]), r=['xb%d' % par, 'idb'], w=['ps2'])
                        op('dve', lambda e: e.tensor_copy(out=xT[:, :, tt * 128:(tt + 1) * 128], in_=pb.rearrange("p (c n) -> p c n", c=8)), r=['ps2'], w=['xT'])
                    S.dma('sp', XT[:, st * 512:(st + 1) * 512].rearrange("(c p) n -> p c n", p=128), xT[:], r=['xT'])
                    for ci in range(9 if KSUB >= 2 else 0):
                        col = ci * 128 if ci < 5 else C_GM + (ci - 5) * 128
                        bk = ci % 2
                        for c in range(8):
                            op('pe', lambda e: e.matmul(ps[:, bk, :], lhsT=wA[:, c, col:col + 128], rhs=xT[:, c, :], start=(c == 0), stop=(c == 7)), r=['wA', 'xT'], w=['ps%d' % bk])
                        if ci < 5:
                            op('act', lambda e: e.activation(out=latT[:, ci, :], in_=ps[:, bk, :], func=AF.Copy), r=['ps%d' % bk], w=['latT'])
                            op('dve', lambda e: e.tensor_tensor(out=sq[:, ci, :], in0=ps[:, bk, :], in1=latT[:, ci, :], op=ALU.mult), r=['ps%d' % bk, 'latT'], w=['sq'])
                        else:
                            op('act', lambda e: e.activation(out=sgm[:, ci - 5, :], in_=ps[:, bk, :], func=AF.Silu), r=['ps%d' % bk], w=['sgm'])
                    S.dma('sp', GMT[:, st * 512:(st + 1) * 512].rearrange("(c p) n -> p c n", p=128), sgm[:], r=['sgm'])
                    for tt in range(4 if KSUB >= 3 else 0):
                        t = st * 4 + tt
                        tok = slice(tt * 128, (tt + 1) * 128)
                        rope_tables(t, 0)
                        if KSUB < 31:
                            continue
                        for c in range(3):
                            op('pe', lambda e: e.matmul(ps[:, 3, 0:1], lhsT=sq[:, c, tok], rhs=ones_b[:, 0:1], start=(c == 0), stop=(c == 2)), r=['sq', 'ones_b'], w=['ps3'])
                        for c in range(2):
                            op('pe', lambda e: e.matmul(ps[:, 3, 1:2], lhsT=sq[:, 3 + c, tok], rhs=ones_b[:, 0:1], start=(c == 0), stop=(c == 1)), r=['sq', 'ones_b'], w=['ps3'])
                        op('act', lambda e: e.activation(out=sm[:, 0:1], in_=ps[:, 3, 0:1], func=AF.Sqrt, bias=RMS_EPS, scale=1.0 / 384.0), r=['ps3'], w=['sm'])
                        op('act', lambda e: e.activation(out=sm[:, 1:2], in_=ps[:, 3, 1:2], func=AF.Sqrt, bias=RMS_EPS, scale=1.0 / 256.0), r=['ps3'], w=['sm'])
                        op('dve', lambda e: e.reciprocal(out=sm[:, 2:4], in_=sm[:, 0:2]), r=['sm'], w=['sm'])
                        op('dve', lambda e: e.tensor_scalar(out=sm[:, 4:5], in0=sm[:, 2:3], scalar1=96.0 ** -0.5, scalar2=None, op0=ALU.mult), r=['sm'], w=['sm'])
                        if KSUB < 32:
                            continue
                        op('dve', lambda e: e.tensor_scalar(out=ctab[:], in0=tab[:, 0, 0:32], scalar1=sm[:, 4:5], scalar2=None, op0=ALU.mult), r=['tab0', 'sm'], w=['ctab'])
                        pq = ps[:, 4:6, :].rearrange("p a b -> p (a b)")
                        for c in range(3):
                            op('pe', lambda e: e.matmul(pq[:, 0:512], lhsT=latT[:, c, tok], rhs=wuq[:, c, 0:512], start=(c == 0), stop=(c == 2)), r=['latT', 'wuq'], w=['ps4'])
                        for c in range(3):
                            op('pe', lambda e: e.matmul(pq[:, 512:768], lhsT=latT[:, c, tok], rhs=wuq[:, c, 512:768], start=(c == 0), stop=(c == 2)), r=['latT', 'wuq'], w=['ps5'])
                        if KSUB < 33:
                            continue
                        pq3 = pq[:, 0:768].rearrange("p (h d) -> p h d", h=8)
                        op('dve', lambda e: e.tensor_scalar(out=Qst[:, :, 0:64], in0=pq3[:, :, 0:64], scalar1=sm[:, 4:5], scalar2=None, op0=ALU.mult), r=['ps4', 'ps5', 'sm'], w=['Qst'])
                        if KSUB < 34:
                            continue
                        rope('pool', pq3[:, :, 64:96], ['ps4', 'ps5'], Qst[:, :, 64:96], ['Qst'], 8, 16, ctab[:, 0:16], ctab[:, 16:32], 'ctab', rtmp, 'rtmp')
                        if KSUB < 35:
                            continue
                        pb = psb(2)
                        for h in range(NH):
                            op('pe', lambda e: e.transpose(out=pb[0:96, h * 128:(h + 1) * 128], in_=Qst[:, h, :], identity=idb[:]), r=['Qst', 'idb'], w=['ps2'])
                        op('act', lambda e: e.activation(out=QTst[0:96, :, tok], in_=pb[0:96, :].rearrange("p (h n) -> p h n", h=8), func=AF.Copy), r=['ps2'], w=['QTst'])
                        if KSUB < 36:
                            continue
                        pkv = ps[:, 6:8, :].rearrange("p a b -> p (a b)")
                        for hb in range(2):
                            for c in range(2):
                                op('pe', lambda e: e.matmul(pkv[:, hb * 512:(hb + 1) * 512], lhsT=latT[:, 3 + c, tok], rhs=wukv[:, c, hb * 512:(hb + 1) * 512], start=(c == 0), stop=(c == 1)), r=['latT', 'wukv'], w=['ps%d' % (6 + hb)])
                        pkv3 = pkv.rearrange("p (h d) -> p h d", h=8)
                        op('dve', lambda e: e.tensor_scalar(out=Kst[:, :, 0:64], in0=pkv3[:, :, 0:64], scalar1=sm[:, 3:4], scalar2=None, op0=ALU.mult), r=['ps6', 'ps7', 'sm'], w=['Kst'])
                        op('dve', lambda e: e.tensor_scalar(out=Vst[:, tt, :, 0:64], in0=pkv3[:, :, 64:128], scalar1=sm[:, 3:4], scalar2=None, op0=ALU.mult), r=['ps6', 'ps7', 'sm'], w=['Vst'])
                        if KSUB < 37:
                            continue
                        for c in range(8):
                            op('pe', lambda e: e.matmul(ps[:, 3, 64:96], lhsT=xT[:, c, tok], rhs=wA[:, c, C_KPE:C_KPE + 32], start=(c == 0), stop=(c == 7)), r=['xT', 'wA'], w=['ps3'])
                        if KSUB < 38:
                            continue
                        rope('pool', ps[:, 3, 64:96].unsqueeze(1), ['ps3'], krot[:, :, :], ['krot'], 1, 16, tab[:, 0, 0:16], tab[:, 0, 16:32], 'tab0', rtmp, 'rtmp')
                        if KSUB < 39:
                            continue
                        op('pool', lambda e: e.tensor_copy(out=Kst[:, :, 64:96], in_=krot[:, 0, :].unsqueeze(1).broadcast_to([128, 8, 32])), r=['krot'], w=['Kst'])
                        if KSUB < 40:
                            continue
                        pb = psb(2)
                        for h in range(NH):
                            op('pe', lambda e: e.transpose(out=pb[0:96, h * 128:(h + 1) * 128], in_=Kst[:, h, :], identity=idb[:]), r=['Kst', 'idb'], w=['ps2'])
                        op('act', lambda e: e.activation(out=KTst[0:96, :, tok], in_=pb[0:96, :].rearrange("p (h n) -> p h n", h=8), func=AF.Copy), r=['ps2'], w=['KTst'])
                    if KSUB < 50:
                        continue
                    S.dma('sp', QT[:, :, st * 512:(st + 1) * 512].rearrange("h d n -> d h n"), QTst[0:96, :, :], r=['QTst'])
                    S.dma('sp', KT[:, :, st * 512:(st + 1) * 512].rearrange("h d n -> d h n"), KTst[0:96, :, :], r=['KTst'])
                    for h in range(NH):
                        S.dma('sp', VV[h, :, st * 4:(st + 1) * 4, :], Vst[:, :, h, :], r=['Vst'])
                S.barrier()

            with contextlib.ExitStack() as es:
              if 'A2' in PH:
                xcb = sb(es, "xcb", [128, 512], BF16)
                Rq = sb(es, "Rq", [128, 8, 64], BF16)
                Rk = sb(es, "Rk", [128, 8, 64], BF16)
                Rk2 = sb(es, "Rk2", [128, 8, 64], BF16)
                Rv = sb(es, "Rv", [128, 8, 64], BF16)
                QrTz = sb(es, "QrTz", [128, 8, 128], BF16)
                QrT2z = sb(es, "QrT2z", [128, 8, 128], BF16)
                KrT = sb(es, "KrT", [128, 4, 128], BF16)
                PT = sb(es, "PT", [128, 8, 128], BF16)
                Sbf = sb(es, "Sbf", [128, 4, 64], BF16)
                onb = sb(es, "onb", [128, 8, 64], BF16)
                wB = sb(es, "wB", [128, 8, 3072], BF16)
                bdr = sb(es, "bdr", [128, 4, 128], BF16)
                bdi = sb(es, "bdi", [128, 4, 128], BF16)
                cw = sb(es, "cw", [128, 4, 4], F32)
                lv = sb(es, "lv", [128, 4, 4], F32)
                cl = sb(es, "cl", [128, 4, 2], F32)
                gng = sb(es, "gng", [128, 4], F32)
                xT = sb(es, "xT2", [128, 2, 8, 512], BF16)
                uT = sb(es, "uT", [128, 4, 516], F32)
                sgl = sb(es, "sgl", [128, 4, 512], BF16)
                sgr = sb(es, "sgr", [128, 4, 512], BF16)
                xc = sb(es, "xc", [128, 512], F32)
                gr = sb(es, "gr", [128, 512], F32)
                gi = sb(es, "gi", [128, 512], F32)
                aa = sb(es, "aa", [128, 512], F32)
                a2 = sb(es, "a2", [128, 512], F32)
                bb = sb(es, "bb", [128, 512], F32)
                hh_ = sb(es, "hh", [128, 4, 512], F32)
                hlast = sb(es, "hlast", [128, 4], F32)
                ylT = sb(es, "ylT", [128, 4, 512], BF16)
                yrT = sb(es, "yrT", [128, 4, 512], BF16)
                Sst = sb(es, "Sst", [128, 4, 64], F32)
                osq = sb(es, "osq", [128, 8, 64], F32)
                otmp = sb(es, "otmp", [128, 8, 64], F32)
                gs = sb(es, "gs", [128, 6, 8], F32)
                rtmp = sb(es, "rtmp2", [128, 4, 256], F32)

                for c in range(8):
                    castload(wB[:, c, :], w_in[l, c * 128:(c + 1) * 128, C_U:C_MG], 3072, ['wB'])
                castload(bdr[:].rearrange("p a b -> p (a b)"), bd_r[l].rearrange("p a b -> p (a b)"), 512, ['bdr'])
                castload(bdi[:].rearrange("p a b -> p (a b)"), bd_i[l].rearrange("p a b -> p (a b)"), 512, ['bdi'])
                S.dma('sp', cw[:], conv_w[l], w=['cw'])
                S.dma('sp', lv[:], lru_vec[l], w=['lv'])
                S.dma('sp', gng[:], gn_g[l], w=['gng'])
                op('act', lambda e: e.activation(out=cl[:, :, 0], in_=lv[:, :, 3], func=AF.Exp, scale=-1.0), r=['lv'], w=['cl'])
                spt = sb(es, "spt", [128, 4, 4], F32)
                op('dve', lambda e: e.tensor_copy(out=spt[:, :, 0], in_=cl[:, :, 0]), r=['cl'], w=['spt'])
                op('act', lambda e: e.activation(out=cl[:, :, 0], in_=cl[:, :, 0], func=AF.Ln, bias=1.0), r=['cl'], w=['cl'])
                op('dve', lambda e: e.tensor_scalar(out=spt[:, :, 1], in0=spt[:, :, 0], scalar1=-0.2, scalar2=0.25, op0=ALU.mult, op1=ALU.add), r=['spt'], w=['spt'])
                for cf in (-1.0 / 3.0, -0.5, -1.0):
                    op('dve', lambda e: e.tensor_tensor(out=spt[:, :, 1], in0=spt[:, :, 1], in1=spt[:, :, 0], op=ALU.mult), r=['spt'], w=['spt'])
                    op('dve', lambda e: e.tensor_scalar(out=spt[:, :, 1], in0=spt[:, :, 1], scalar1=-1.0, scalar2=-cf, op0=ALU.mult, op1=ALU.add), r=['spt'], w=['spt'])
                op('dve', lambda e: e.tensor_tensor(out=spt[:, :, 1], in0=spt[:, :, 1], in1=spt[:, :, 0], op=ALU.mult), r=['spt'], w=['spt'])
                op('dve', lambda e: e.tensor_scalar(out=spt[:, :, 2], in0=spt[:, :, 0], scalar1=0.1, scalar2=None, op0=ALU.is_lt), r=['spt'], w=['spt'])
                op('dve', lambda e: e.tensor_tensor(out=spt[:, :, 3], in0=spt[:, :, 1], in1=cl[:, :, 0], op=ALU.subtract), r=['spt', 'cl'], w=['spt'])
                op('dve', lambda e: e.tensor_tensor(out=spt[:, :, 3], in0=spt[:, :, 3], in1=spt[:, :, 2], op=ALU.mult), r=['spt'], w=['spt'])
                op('dve', lambda e: e.tensor_tensor(out=cl[:, :, 0], in0=cl[:, :, 0], in1=spt[:, :, 3], op=ALU.add), r=['spt', 'cl'], w=['cl'])
                op('dve', lambda e: e.tensor_scalar(out=cl[:, :, 1], in0=cl[:, :, 0], scalar1=-16.0, scalar2=None, op0=ALU.mult), r=['cl'], w=['cl'])
                op('dve', lambda e: e.tensor_scalar(out=cl[:, :, 0], in0=cl[:, :, 0], scalar1=-8.0, scalar2=None, op0=ALU.mult), r=['cl'], w=['cl'])
                op('pool', lambda e: e.memset(uT[:], 0.0), w=['uT0', 'uT1', 'uT2', 'uT3'])
                op('pool', lambda e: e.memset(hlast[:], 0.0), w=['hlast'])
                op('pool', lambda e: e.memset(Sst[:], 0.0), w=['Sst'])
                op('pool', lambda e: e.memset(Sbf[:], 0.0), w=['Sbf'])
                op('pool', lambda e: e.memset(QrTz[:], 0.0), w=['QrTz'])
                op('pool', lambda e: e.memset(QrT2z[:], 0.0), w=['QrT2z'])

                def load_xT(st):
                    S.dma('sp', xT[:, st % 2, :, :], XT[:, st * 512:(st + 1) * 512].rearrange("(c p) n -> p c n", p=128), w=['xT%d' % (st % 2)])
                load_xT(0)
                for st in range(NST):
                    xp = st % 2
                    kx = 'xT%d' % xp
                    if st + 1 < NST:
                        load_xT(st + 1)
                    for ci in range(12):
                        if ci < 8:
                            col = ci * 128
                        else:
                            col = (C_GR - C_U) + (ci - 8) * 128
                        bk = ci % 2
                        if ci < 4:
                            op('pool', lambda e: e.tensor_copy(out=uT[:, ci, 0:3], in_=uT[:, ci, 512:515]), r=['uT%d' % ci], w=['uT%d' % ci])
                        for c in range(8):
                            op('pe', lambda e: e.matmul(ps[:, bk, :], lhsT=wB[:, c, col:col + 128], rhs=xT[:, xp, c, :], start=(c == 0), stop=(c == 7)), r=['wB', kx], w=['ps%d' % bk])
                        if ci < 4:
                            op('act', lambda e: e.activation(out=uT[:, ci, 3:515], in_=ps[:, bk, :], func=AF.Copy), r=['ps%d' % bk], w=['uT%d' % ci])
                        elif ci < 8:
                            op('act', lambda e: e.activation(out=sgl[:, ci - 4, :], in_=ps[:, bk, :], func=AF.Silu), r=['ps%d' % bk], w=['sgl'])
                        else:
                            op('act', lambda e: e.activation(out=sgr[:, ci - 8, :], in_=ps[:, bk, :], func=AF.Silu), r=['ps%d' % bk], w=['sgr'])
                    for ci in range(4 if KSUB >= 2 else 0):
                        ku = 'uT%d' % ci
                        op('dve', lambda e: e.tensor_scalar(out=xc[:], in0=uT[:, ci, 0:512], scalar1=cw[:, ci, 0:1], scalar2=lv[:, ci, 0:1], op0=ALU.mult, op1=ALU.add), r=[ku, 'cw', 'lv'], w=['xc'])
                        for k in range(1, 4):
                            op('dve', lambda e: e.scalar_tensor_tensor(out=xc[:], in0=uT[:, ci, k:k + 512], scalar=cw[:, ci, k:k + 1], in1=xc[:], op0=ALU.mult, op1=ALU.add), r=[ku, 'cw', 'xc'], w=['xc'])
                        op('pool', lambda e: e.tensor_copy(out=xcb[:], in_=xc[:]), r=['xc'], w=['xcb'])
                        op('pe', lambda e: e.matmul(ps[:, 2, :], lhsT=bdr[:, ci, :], rhs=xcb[:], start=True, stop=True), r=['bdr', 'xcb'], w=['ps2'])
                        op('pe', lambda e: e.matmul(ps[:, 3, :], lhsT=bdi[:, ci, :], rhs=xcb[:], start=True, stop=True), r=['bdi', 'xcb'], w=['ps3'])
                        op('act', lambda e: e.activation(out=gr[:], in_=ps[:, 2, :], func=AF.Sigmoid, bias=lv[:, ci, 1:2]), r=['ps2', 'lv'], w=['gr'])
                        op('act', lambda e: e.activation(out=gi[:], in_=ps[:, 3, :], func=AF.Sigmoid, bias=lv[:, ci, 2:3]), r=['ps3', 'lv'], w=['gi'])
                        op('dve', lambda e: e.tensor_scalar(out=gr[:], in0=gr[:], scalar1=cl[:, ci, 0:1], scalar2=None, op0=ALU.mult), r=['gr', 'cl'], w=['gr'])
                        op('act', lambda e: e.activation(out=aa[:], in_=gr[:], func=AF.Exp), r=['gr'], w=['aa'])
                        op('act', lambda e: e.activation(out=a2[:], in_=gr[:], func=AF.Exp, scale=2.0), r=['gr'], w=['a2'])
                        op('act', lambda e: e.activation(out=a2[:], in_=a2[:], func=AF.Sqrt, bias=1.0, scale=-1.0), r=['a2'], w=['a2'])
                        op('pool', lambda e: e.tensor_tensor(out=gi[:], in0=gi[:], in1=xc[:], op=ALU.mult), r=['gi', 'xc'], w=['gi'])
                        op('dve', lambda e: e.tensor_tensor(out=bb[:], in0=gi[:], in1=a2[:], op=ALU.mult), r=['gi', 'a2'], w=['bb'])
                        op('dve', lambda e: e.tensor_tensor_scan(out=hh_[:, ci, :], data0=aa[:], data1=bb[:], initial=hlast[:, ci:ci + 1], op0=ALU.mult, op1=ALU.add), r=['aa', 'bb', 'hlast'], w=['hh%d' % ci])
                        op('pool', lambda e: e.tensor_copy(out=hlast[:, ci:ci + 1], in_=hh_[:, ci, 511:512]), r=['hh%d' % ci], w=['hlast'])
                        op('pool', lambda e: e.tensor_tensor(out=ylT[:, ci, :], in0=hh_[:, ci, :], in1=sgl[:, ci, :], op=ALU.mult), r=['hh%d' % ci, 'sgl'], w=['ylT'])
                    S.dma('sp', YTL[:, st * 512:(st + 1) * 512].rearrange("(c p) n -> p c n", p=128), ylT[:], r=['ylT'])
                    for tt in range(4 if KSUB >= 3 else 0):
                        t = st * 4 + tt
                        tok = slice(tt * 128, (tt + 1) * 128)
                        rope_tables(t, 1)
                        if KSUB < 61:
                            continue
                        for j in range(3):
                            col = (C_RQ - C_U) + j * 512
                            for c in range(8):
                                op('pe', lambda e: e.matmul(ps[:, 4 + j, :], lhsT=xT[:, xp, c, tok], rhs=wB[:, c, col:col + 512], start=(c == 0), stop=(c == 7)), r=[kx, 'wB'], w=['ps%d' % (4 + j)])
                        if KSUB < 62:
                            continue
                        rope('dve', ps[:, 4, :].rearrange("p (h d) -> p h d", h=8), ['ps4'], Rq[:], ['Rq'], 8, 32, tab[:, 1, 32:64], tab[:, 1, 64:96], 'tab1', rtmp, 'rtmpa')
                        if KSUB < 63:
                            continue
                        rope('dve', ps[:, 5, :].rearrange("p (h d) -> p h d", h=8), ['ps5'], Rk[:], ['Rk'], 8, 32, tab[:, 1, 96:128], tab[:, 1, 128:160], 'tab1', rtmp, 'rtmpa')
                        if KSUB < 64:
                            continue
                        op('act', lambda e: e.activation(out=Rv[:].rearrange("p h d -> p (h d)"), in_=ps[:, 6, :], func=AF.Copy), r=['ps6'], w=['Rv'])
                        op('dve', lambda e: e.tensor_tensor(out=Rk2[:], in0=Rk[:], in1=kw[:, :].unsqueeze(2).broadcast_to([128, 8, 64]), op=ALU.mult), r=['Rk', 'kw'], w=['Rk2'])
                        if KSUB < 65:
                            continue
                        pb = psb(1)
                        for p_ in range(4):
                            op('pe', lambda e: e.matmul(ps[:, 1, p_ * 128:(p_ + 1) * 128], lhsT=Rq[:, 2 * p_:2 * p_ + 2, :].rearrange("p h d -> p (h d)"), rhs=idb[:], start=True, stop=True), r=['Rq', 'idb'], w=['ps1'])
                        pbk = psb(7)
                        for p_ in range(4):
                            op('pe', lambda e: e.matmul(ps[:, 7, p_ * 128:(p_ + 1) * 128], lhsT=Rk[:, 2 * p_:2 * p_ + 2, :].rearrange("p h d -> p (h d)"), rhs=idb[:], start=True, stop=True), r=['Rk', 'idb'], w=['ps7'])
                        if KSUB < 66:
                            continue
                        pbq = ps[:, 1, :].rearrange("p (a n) -> p a n", a=4)
                        for hh in range(2):
                            rows = slice(hh * 64, (hh + 1) * 64)
                            qz = QrTz[:].rearrange("p (a b) n -> p a b n", b=2)[rows, :, hh, :]
                            q2z = QrT2z[:].rearrange("p (a b) n -> p a b n", b=2)[rows, :, hh, :]
                            op('act', lambda e: e.activation(out=qz, in_=pbq[rows, :, :], func=AF.Copy), r=['ps1'], w=['QrTz'])
                            op('dve', lambda e: e.tensor_tensor(out=q2z, in0=pbq[rows, :, :], in1=qwT[rows, :, :], op=ALU.mult), r=['ps1', 'qwT'], w=['QrT2z'])
                        op('act', lambda e: e.activation(out=KrT[:].rearrange("p a b -> p (a b)"), in_=ps[:, 7, :], func=AF.Copy), r=['ps7'], w=['KrT'])
                        if KSUB < 67:
                            continue
                        psc = ps[:, 2:4, :].rearrange("p a b -> p (a b)")
                        for h in range(NH):
                            p_ = h // 2
                            op('pe', lambda e: e.matmul(psc[:, h * 128:(h + 1) * 128], lhsT=KrT[:, p_, :], rhs=QrTz[:, h, :], start=True, stop=True), r=['KrT', 'QrTz'], w=['ps%d' % (2 + h // 4)])
                        op('dve', lambda e: e.tensor_tensor(out=PT[:].rearrange("p a b -> p (a b)"), in0=psc, in1=DT[:].rearrange("p a b -> p (a b)"), op=ALU.mult), r=['ps2', 'ps3', 'DT'], w=['PT'])
                        if KSUB < 68:
                            continue
                        po = ps[:, 4, :].rearrange("p (h d) -> p h d", h=8)
                        for h in range(NH):
                            p_ = h // 2
                            op('pe', lambda e: e.matmul(po[:, h, :], lhsT=PT[:, h, :], rhs=Rv[:, h, :], start=True, stop=False), r=['PT', 'Rv'], w=['ps4'])
                            op('pe', lambda e: e.matmul(po[:, h, :], lhsT=QrT2z[:, h, :], rhs=Sbf[:, p_, :], start=False, stop=True), r=['QrT2z', 'Sbf'], w=['ps4'])
                        if KSUB < 69:
                            continue
                        pkv = ps[:, 6, :].rearrange("p (a e) -> p a e", a=4)
                        for p_ in range(4):
                            op('pe', lambda e: e.matmul(pkv[:, p_, :], lhsT=Rk2[:, 2 * p_:2 * p_ + 2, :].rearrange("p h d -> p (h d)"), rhs=Rv[:, 2 * p_:2 * p_ + 2, :].rearrange("p h d -> p (h d)"), start=True, stop=True), r=['Rk2', 'Rv'], w=['ps6'])
                        op('dve', lambda e: e.tensor_tensor(out=Sst[:], in0=Sst[:], in1=gCt[:, :].unsqueeze(2).broadcast_to([128, 4, 64]), op=ALU.mult), r=['Sst', 'gCt'], w=['Sst'])
                        for hh in range(2):
                            rows = slice(hh * 64, (hh + 1) * 64)
                            op('dve', lambda e: e.tensor_tensor(out=Sst[rows, :, :], in0=Sst[rows, :, :], in1=pkv[rows, :, hh * 64:(hh + 1) * 64], op=ALU.add), r=['Sst', 'ps6'], w=['Sst'])
                        op('pool', lambda e: e.tensor_copy(out=Sbf[:], in_=Sst[:]), r=['Sst'], w=['Sbf'])
                        if KSUB < 70:
                            continue
                        op('dve', lambda e: e.tensor_reduce(out=gs[:, 0, :], in_=po, axis=AX.X, op=ALU.add), r=['ps4'], w=['gs'])
                        op('act', lambda e: e.activation(out=osq[:], in_=po, func=AF.Square), r=['ps4'], w=['osq'])
                        op('dve', lambda e: e.tensor_reduce(out=gs[:, 1, :], in_=osq[:], axis=AX.X, op=ALU.add), r=['osq'], w=['gs'])
                        op('dve', lambda e: e.tensor_scalar(out=gs[:, 2, :], in0=gs[:, 0, :], scalar1=1.0 / 64.0, scalar2=None, op0=ALU.mult), r=['gs'], w=['gs'])
                        op('dve', lambda e: e.tensor_tensor(out=gs[:, 3, :], in0=gs[:, 2, :], in1=gs[:, 2, :], op=ALU.mult), r=['gs'], w=['gs'])
                        op('dve', lambda e: e.scalar_tensor_tensor(out=gs[:, 3, :], in0=gs[:, 1, :], scalar=1.0 / 64.0, in1=gs[:, 3, :], op0=ALU.mult, op1=ALU.subtract), r=['gs'], w=['gs'])
                        op('act', lambda e: e.activation(out=gs[:, 4, :], in_=gs[:, 3, :], func=AF.Sqrt, bias=LN_EPS, scale=1.0), r=['gs'], w=['gs'])
                        op('dve', lambda e: e.reciprocal(out=gs[:, 5, :], in_=gs[:, 4, :]), r=['gs'], w=['gs'])
                        op('dve', lambda e: e.tensor_tensor(out=otmp[:], in0=po, in1=gs[:, 2, :].unsqueeze(2).broadcast_to([128, 8, 64]), op=ALU.subtract), r=['ps4', 'gs'], w=['otmp'])
                        op('pool', lambda e: e.tensor_tensor(out=onb[:], in0=otmp[:], in1=gs[:, 5, :].unsqueeze(2).broadcast_to([128, 8, 64]), op=ALU.mult), r=['otmp', 'gs'], w=['onb'])
                        if KSUB < 71:
                            continue
                        pb6 = psb(1)
                        for p_ in range(4):
                            op('pe', lambda e: e.matmul(ps[:, 1, p_ * 128:(p_ + 1) * 128], lhsT=onb[:, 2 * p_:2 * p_ + 2, :].rearrange("p h d -> p (h d)"), rhs=idb[:], start=True, stop=True), r=['onb', 'idb'], w=['ps1'])
                        for p_ in range(4):
                            op('dve', lambda e: e.scalar_tensor_tensor(out=yrT[:, p_, tok], in0=ps[:, 1, p_ * 128:(p_ + 1) * 128], scalar=gng[:, p_:p_ + 1], in1=sgr[:, p_, tok], op0=ALU.mult, op1=ALU.mult), r=['ps1', 'gng', 'sgr'], w=['yrT'])
                    S.dma('sp', YTR[:, st * 512:(st + 1) * 512].rearrange("(c p) n -> p c n", p=128), yrT[:], r=['yrT'])
                S.barrier()

            with contextlib.ExitStack() as es:
              if 'B' in PH:
                KTh = sb(es, "KTh", [128, 2, SEQ], BF16)
                QTh = sb(es, "QTh", [128, 2, SEQ], BF16)
                Vh = sb(es, "Vh", [128, 2, NT, 65], BF16)
                PTb = sb(es, "PTb", [128, 4, 512], BF16)
                ob = sb(es, "ob", [128, 2, 4, 64], BF16)
                rs = sb(es, "rs", [128, 2, 4], F32)

                def load_head(h):
                    hp = h % 2
                    CH = 2048
                    for c0 in range(0, SEQ, CH):
                        c1 = min(SEQ, c0 + CH)
                        S.dma('sp', KTh[0:96, hp, c0:c1], KT[h, :, c0:c1], w=['KTh%d' % hp])
                        S.dma('sp', QTh[0:96, hp, c0:c1], QT[h, :, c0:c1], w=['QTh%d' % hp])
                    for b0 in range(0, NT, 16):
                        b1 = min(NT, b0 + 16)
                        S.dma('sp', Vh[:, hp, b0:b1, :], VV[h, :, b0:b1, :], w=['Vh%d' % hp])
                load_head(0)
                pi = 0
                for h in range(NH):
                    hp = h % 2
                    if h + 1 < NH:
                        load_head(h + 1)
                    for g in range(NST):
                        nkb = 4 * g + 4
                        ops_ = g % 2
                        pov = ps[:, 4 + ops_, 0:260].rearrange("p (a e) -> p a e", a=4)
                        for kb in range(nkb):
                            j = kb - 4 * g
                            q0 = max(j, 0) * 128
                            nq = 512 - q0
                            sbk = pi % 4
                            pi += 1
                            ksl = slice(kb * 128, (kb + 1) * 128)
                            op('pe', lambda e: e.matmul(ps[:, sbk, q0:512], lhsT=KTh[0:96, hp, ksl], rhs=QTh[0:96, hp, g * 512 + q0:(g + 1) * 512], start=True, stop=True), r=['KTh%d' % hp, 'QTh%d' % hp], w=['ps%d' % sbk])
                            op('act', lambda e: e.activation(out=PTb[:, sbk, q0:512], in_=ps[:, sbk, q0:512], func=AF.Exp), r=['ps%d' % sbk], w=['PTb%d' % sbk])
                            if j >= 0:
                                op('pool', lambda e: e.tensor_tensor(out=PTb[:, sbk, q0:q0 + 128], in0=PTb[:, sbk, q0:q0 + 128], in1=cmask[:], op=ALU.mult), r=['PTb%d' % sbk, 'cmask'], w=['PTb%d' % sbk])
                            for qb in range(max(j, 0), 4):
                                last = (kb == 4 * g + qb)
                                op('pe', lambda e: e.matmul(pov[:, qb, :], lhsT=PTb[:, sbk, qb * 128:(qb + 1) * 128], rhs=Vh[:, hp, kb, :], start=(kb == 0 and qb == 0), stop=last, skip_group_check=True), r=['PTb%d' % sbk, 'Vh%d' % hp], w=['ps%d' % (4 + ops_)])
                        op('dve', lambda e: e.reciprocal(out=rs[:, ops_, :], in_=pov[:, :, 64]), r=['ps%d' % (4 + ops_)], w=['rs%d' % ops_])
                        op('dve', lambda e: e.tensor_tensor(out=ob[:, ops_, :, :], in0=pov[:, :, 0:64], in1=rs[:, ops_, :].unsqueeze(2).broadcast_to([128, 4, 64]), op=ALU.mult), r=['ps%d' % (4 + ops_), 'rs%d' % ops_], w=['ob%d' % ops_])
                        S.dma('sp', OO[g * 512:(g + 1) * 512, h * 64:(h + 1) * 64].rearrange("(a p) e -> p a e", p=128), ob[:, ops_, :, :], r=['ob%d' % ops_])
                S.barrier()

            with contextlib.ExitStack() as es:
              if 'C' in PH:
                wM = sb(es, "wM", [128, 8, 3072], BF16)
                wBr = sb(es, "wBr", [128, 12, 1024], BF16)
                wO = sb(es, "wO", [128, 8, 1024], BF16)
                bm = sb(es, "bm", [128, 24], F32)
                lng = sb(es, "lng", [128, 1024], F32)
                lnb = sb(es, "lnb", [128, 1024], F32)
                xT = sb(es, "xT3", [128, 8, 512], BF16)
                gt = sb(es, "gt", [128, 3, 512], BF16)
                yT = sb(es, "yT", [128, 12, 512], BF16)
                sgm = sb(es, "sgm3", [128, 4, 512], BF16)
                ot = sb(es, "ot", [128, 4, 512], BF16)
                mixT = sb(es, "mixT", [128, 8, 512], BF16)
                m0 = sb(es, "m0", [128, 512], F32)
                m1 = sb(es, "m1", [128, 512], F32)
                xr = sb(es, "xr", [128, 2, 1024], F32)
                zz = sb(es, "zz", [128, 1024], F32)
                st6 = sb(es, "st6", [128, 2, 6], F32)
                mv = sb(es, "mv", [128, 4], F32)

                for c in range(8):
                    castload(wM[:, c, :], w_in[l, c * 128:(c + 1) * 128, C_MG:IN_COLS], 3072, ['wM'])
                    castload(wO[:, c, :], w_out[l, c * 128:(c + 1) * 128, :], 1024, ['wO'])
                for b in range(3):
                    for c in range(4):
                        castload(wBr[:, b * 4 + c, :], w_branch[l, b, c * 128:(c + 1) * 128, :], 1024, ['wBr'])
                S.dma('sp', bm[:], b_merge[l], w=['bm'])
                S.dma('sp', lng[:], ln_g[l].partition_broadcast(128), w=['lng'])
                S.dma('sp', lnb[:], ln_b[l].partition_broadcast(128), w=['lnb'])

                for st in range(NST):
                    cols = slice(st * 512, (st + 1) * 512)
                    S.dma('sp', xT[:], XT[:, cols].rearrange("(c p) n -> p c n", p=128), w=['xT'])
                    S.dma('sp', yT[:, 4:8, :], YTL[:, cols].rearrange("(c p) n -> p c n", p=128), w=['yT1'])
                    S.dma('sp', yT[:, 8:12, :], YTR[:, cols].rearrange("(c p) n -> p c n", p=128), w=['yT2'])
                    S.dma('sp', sgm[:], GMT[:, cols].rearrange("(c p) n -> p c n", p=128), w=['sgm'])
                    S.dma('sp', ot[:], OO[cols, :].rearrange("(a p) e -> p a e", p=128), w=['ot'])
                    for tt in range(4):
                        pb = psb(2 + tt % 2)
                        kps = 'ps%d' % (2 + tt % 2)
                        for c in range(4):
                            op('pe', lambda e: e.transpose(out=pb[:, c * 128:(c + 1) * 128], in_=ot[:, tt, c * 128:(c + 1) * 128], identity=idb[:]), r=['ot', 'idb'], w=[kps])
                        op('dve', lambda e: e.tensor_tensor(out=yT[:, 0:4, tt * 128:(tt + 1) * 128], in0=pb[:, 0:512].rearrange("p (c n) -> p c n", c=4), in1=sgm[:, :, tt * 128:(tt + 1) * 128], op=ALU.mult), r=[kps, 'sgm'], w=['yT0'])
                    for dc in range(8):
                        for b in range(3):
                            ci = b * 8 + dc
                            bk = (dc * 3 + b) % 2
                            for c in range(8):
                                op('pe', lambda e: e.matmul(ps[:, bk, :], lhsT=wM[:, c, ci * 128:(ci + 1) * 128], rhs=xT[:, c, :], start=(c == 0), stop=(c == 7)), r=['wM', 'xT'], w=['ps%d' % bk])
                            op('act', lambda e: e.activation(out=gt[:, b, :], in_=ps[:, bk, :], func=AF.Sigmoid, bias=bm[:, ci:ci + 1]), r=['ps%d' % bk, 'bm'], w=['gt%d' % b])
                        for b in range(3):
                            bk = 4 + b
                            for c in range(4):
                                op('pe', lambda e: e.matmul(ps[:, bk, :], lhsT=wBr[:, b * 4 + c, dc * 128:(dc + 1) * 128], rhs=yT[:, b * 4 + c, :], start=(c == 0), stop=(c == 3)), r=['wBr', 'yT%d' % b], w=['ps%d' % bk])
                        op('dve', lambda e: e.tensor_tensor(out=m0[:], in0=ps[:, 4, :], in1=gt[:, 0, :], op=ALU.mult), r=['ps4', 'gt0'], w=['m0'])
                        op('dve', lambda e: e.tensor_tensor(out=m1[:], in0=ps[:, 5, :], in1=gt[:, 1, :], op=ALU.mult), r=['ps5', 'gt1'], w=['m1'])
                        op('pool', lambda e: e.tensor_tensor(out=m0[:], in0=m0[:], in1=m1[:], op=ALU.add), r=['m0', 'm1'], w=['m0'])
                        op('dve', lambda e: e.tensor_tensor(out=m1[:], in0=ps[:, 6, :], in1=gt[:, 2, :], op=ALU.mult), r=['ps6', 'gt2'], w=['m1'])
                        op('pool', lambda e: e.tensor_tensor(out=mixT[:, dc, :], in0=m0[:], in1=m1[:], op=ALU.add), r=['m0', 'm1'], w=['mixT'])
                    for tt in range(4):
                        t = st * 4 + tt
                        tok = slice(tt * 128, (tt + 1) * 128)
                        par = t % 2
                        kxr = 'xr%d' % par
                        S.dma('sp', xr[:, par, :], xsrc[t * 128:(t + 1) * 128, :], w=[kxr])
                        for hb in range(2):
                            bk = 2 + hb
                            for c in range(8):
                                op('pe', lambda e: e.matmul(ps[:, bk, :], lhsT=mixT[:, c, tok], rhs=wO[:, c, hb * 512:(hb + 1) * 512], start=(c == 0), stop=(c == 7)), r=['mixT', 'wO'], w=['ps%d' % bk])
                            op('dve', lambda e: e.scalar_tensor_tensor(out=zz[:, hb * 512:(hb + 1) * 512], in0=xr[:, par, hb * 512:(hb + 1) * 512], scalar=ALPHA, in1=ps[:, bk, :], op0=ALU.mult, op1=ALU.add), r=[kxr, 'ps%d' % bk], w=['zz'])
                            op('dve', lambda e: e.bn_stats(out=st6[:, hb, :], in_=zz[:, hb * 512:(hb + 1) * 512]), r=['zz'], w=['st6'])
                        op('dve', lambda e: e.bn_aggr(out=mv[:, 0:2], in_=st6[:].rearrange("p a b -> p (a b)")), r=['st6'], w=['mv'])
                        op('act', lambda e: e.activation(out=mv[:, 2:3], in_=mv[:, 1:2], func=AF.Sqrt, bias=LN_EPS, scale=1.0), r=['mv'], w=['mv'])
                        op('dve', lambda e: e.reciprocal(out=mv[:, 3:4], in_=mv[:, 2:3]), r=['mv'], w=['mv'])
                        op('dve', lambda e: e.tensor_scalar(out=zz[:], in0=zz[:], scalar1=mv[:, 0:1], scalar2=mv[:, 3:4], op0=ALU.subtract, op1=ALU.mult), r=['zz', 'mv'], w=['zz'])
                        op('pool', lambda e: e.tensor_tensor(out=zz[:], in0=zz[:], in1=lng[:], op=ALU.mult), r=['zz', 'lng'], w=['zz'])
                        op('pool', lambda e: e.tensor_tensor(out=xr[:, par, :], in0=zz[:], in1=lnb[:], op=ALU.add), r=['zz', 'lnb', kxr], w=[kxr])
                        S.dma('sp', ydst[t * 128:(t + 1) * 128, :], xr[:, par, :], r=[kxr])
                S.barrier()
        S.finish('sp')
    return nc


_CACHE = {}


def _prep_inputs(inp, b, DEPTH):
    f = np.float32

    def fm(v, n):
        return np.ascontiguousarray(np.asarray(v, f).reshape(DEPTH, n, 128).transpose(0, 2, 1))

    def bd(w):
        w = np.asarray(w, f)
        o = np.zeros((DEPTH, 128, 4, 128), f)
        for h in range(8):
            c, hh = h // 2, h % 2
            o[:, hh * 64:(hh + 1) * 64, c, hh * 64:(hh + 1) * 64] = w[:, h]
        return o

    conv_w = np.asarray(inp['lru_conv_w'], f)
    cw = np.ascontiguousarray(conv_w.reshape(DEPTH, 4, 4, 128).transpose(0, 3, 2, 1))
    lv = np.stack([fm(inp['lru_conv_b'], 4), fm(inp['lru_b_r'], 4), fm(inp['lru_b_i'], 4), fm(inp['lru_lambda'], 4)], axis=-1)
    return {
        "x": np.ascontiguousarray(np.asarray(inp['x'], f)[b]),
        "pos": np.ascontiguousarray(np.asarray(inp['positions'], np.int32)[b].reshape(-1, 1)),
        "w_in": np.ascontiguousarray(np.asarray(inp['w_in'], f)),
        "b_merge": fm(inp['b_merge'], 24),
        "q_norm": fm(inp['mla_q_norm'], 3),
        "w_uq": np.ascontiguousarray(np.asarray(inp['mla_w_uq'], f)),
        "kv_norm": fm(inp['mla_kv_norm'], 2),
        "w_ukv": np.ascontiguousarray(np.asarray(inp['mla_w_ukv'], f)),
        "conv_w": cw,
        "lru_vec": np.ascontiguousarray(lv),
        "bd_r": bd(inp['lru_w_r']),
        "bd_i": bd(inp['lru_w_i']),
        "gn_g": fm(inp['ret_gn_g'], 4),
        "w_branch": np.ascontiguousarray(np.asarray(inp['w_branch'], f)),
        "w_out": np.ascontiguousarray(np.asarray(inp['w_out'], f)),
        "ln_g": np.ascontiguousarray(np.asarray(inp['ln_g'], f)),
        "ln_b": np.ascontiguousarray(np.asarray(inp['ln_b'], f)),
    }


def kernel(**inputs):
    x = np.asarray(inputs['x'])
    B, SEQ, _ = x.shape
    DEPTH = np.asarray(inputs['w_in']).shape[0]
    key = (SEQ, DEPTH)
    if key not in _CACHE:
        _CACHE[key] = build(SEQ, DEPTH)
    nc = _CACHE[key]
    ncores = 8
    in_maps = [_prep_inputs(inputs, c % B, DEPTH) for c in range(ncores)]
    res = run_bass_kernel_spmd(nc, in_maps, core_ids=list(range(ncores)))
    out = np.stack([res.results[b]["y"] for b in range(B)], axis=0)
    return out.astype(np.float32)
```
